# Optimizing a Trainium2 kernel written in Bass

```python
import jax, jax.numpy as jnp
from jax import lax
import numpy as np

D_MODEL = 4096
BATCH = 2
SEQ = 4096
DEPTH = 2

N_MIXERS = 2
D_FF = 11008
CONV_WIDTH = 31
HEAD_DIM = 128
HEADS_PER_GROUP = D_MODEL // HEAD_DIM
ATTN_GROUPS = ((128, 1), (512, 4), (2048, 16))
N_GROUPS = len(ATTN_GROUPS)
QKV_WIDTH = 3 * N_GROUPS * HEADS_PER_GROUP * HEAD_DIM
ROPE_THETA = 10000.0
RMS_EPS = 1e-6
LN_EPS = 1e-5
NEG_INF = -1e30

kernel_name = "hybrid_conformer_conv_dilated_attn_macaron"


def rms_norm(x, g):
    xf = x.astype(jnp.float32)
    y = xf * lax.rsqrt(jnp.mean(xf * xf, axis=-1, keepdims=True) + RMS_EPS)
    return (y * g.astype(jnp.float32)).astype(x.dtype)


def swiglu(x, wg, wu, wd):
    return (jax.nn.silu(x @ wg) * (x @ wu)) @ wd


def conformer_conv(x, pw1, pw1_b, dw, dw_b, ln_g, ln_b, pw2, pw2_b):
    h = x @ pw1 + pw1_b
    a, b = jnp.split(h, 2, axis=-1)
    h = a * jax.nn.sigmoid(b)
    c = h.shape[-1]
    h = lax.conv_general_dilated(
        h, dw[:, None, :], window_strides=(1,),
        padding=[(CONV_WIDTH // 2, CONV_WIDTH // 2)],
        dimension_numbers=('NWC', 'WIO', 'NWC'),
        feature_group_count=c) + dw_b
    hf = h.astype(jnp.float32)
    mu = jnp.mean(hf, axis=-1, keepdims=True)
    var = jnp.mean(jnp.square(hf - mu), axis=-1, keepdims=True)
    hf = (hf - mu) * lax.rsqrt(var + LN_EPS) * ln_g.astype(jnp.float32) + ln_b.astype(jnp.float32)
    h = jax.nn.silu(hf).astype(x.dtype)
    return h @ pw2 + pw2_b


def rope_tables(s):
    inv = ROPE_THETA ** (-jnp.arange(0, HEAD_DIM, 2, dtype=jnp.float32) / HEAD_DIM)
    ang = jnp.arange(s, dtype=jnp.float32)[:, None] * inv[None, :]
    return jnp.cos(ang), jnp.sin(ang)


def apply_rope(t, cos, sin):
    tf = t.astype(jnp.float32)
    t1, t2 = jnp.split(tf, 2, axis=-1)
    c = cos[None, :, None, None, :]
    s = sin[None, :, None, None, :]
    return jnp.concatenate([t1 * c - t2 * s, t2 * c + t1 * s], axis=-1).astype(t.dtype)


def dilated_band_attention(q, k, v, dilation, radius):
    b, s, h, dh = q.shape
    L = s // dilation
    nb = -(-L // radius)
    lp = nb * radius

    def to_classes(t):
        return t.reshape(b, L, dilation, h, dh).transpose(0, 2, 1, 3, 4)

    qc = jnp.pad(to_classes(q), ((0, 0), (0, 0), (0, lp - L), (0, 0), (0, 0)))
    kpad = ((0, 0), (0, 0), (radius, lp - L + radius), (0, 0), (0, 0))
    kc = jnp.pad(to_classes(k), kpad)
    vc = jnp.pad(to_classes(v), kpad)
    qb = qc.reshape(b, dilation, nb, radius, h, dh)

    def windows(t):
        tb = t.reshape(b, dilation, nb + 2, radius, h, dh)
        return jnp.concatenate([tb[:, :, :-2], tb[:, :, 1:-1], tb[:, :, 2:]], axis=3)

    kw, vw = windows(kc), windows(vc)
    blk = jnp.arange(nb)[:, None, None]
    qidx = blk * radius + jnp.arange(radius)[None, :, None]
    kidx = (blk - 1) * radius + jnp.arange(3 * radius)[None, None, :]
    mask = (jnp.abs(kidx - qidx) <= radius) & (kidx >= 0) & (kidx < L)

    sc = jnp.einsum('bcnqhd,bcnkhd->bcnhqk', qb, kw,
                    preferred_element_type=jnp.float32) * (dh ** -0.5)
    sc = jnp.where(mask[:, None], sc, NEG_INF)
    m = jnp.max(sc, axis=-1, keepdims=True)
    p = jnp.exp(sc - m)
    den = jnp.sum(p, axis=-1)
    o = jnp.einsum('bcnhqk,bcnkhd->bcnqhd', p, vw.astype(jnp.float32))
    o = o / jnp.swapaxes(den, -1, -2)[..., None]
    lse = jnp.swapaxes(m[..., 0] + jnp.log(den), -1, -2)

    def from_classes(t):
        t = t.reshape((b, dilation, lp) + t.shape[4:])[:, :, :L]
        t = jnp.moveaxis(t, 1, 2)
        return t.reshape((b, s) + t.shape[3:])

    return from_classes(o), from_classes(lse)


def dilated_attention_mixer(x, w_qkv, w_o, cos, sin):
    b, s, _ = x.shape
    qkv = (x @ w_qkv).reshape(b, s, N_GROUPS, 3, HEADS_PER_GROUP, HEAD_DIM)
    q = apply_rope(qkv[:, :, :, 0], cos, sin)
    k = apply_rope(qkv[:, :, :, 1], cos, sin)
    v = qkv[:, :, :, 2]
    outs, lses = [], []
    for g, (window, dilation) in enumerate(ATTN_GROUPS):
        o, l = dilated_band_attention(q[:, :, g], k[:, :, g], v[:, :, g],
                                      dilation, window // (2 * dilation))
        outs.append(o)
        lses.append(l)
    w = jax.nn.softmax(jnp.stack(lses, axis=0), axis=0)
    o = jnp.sum(w[..., None] * jnp.stack(outs, axis=0), axis=0)
    return o.reshape(b, s, HEADS_PER_GROUP * HEAD_DIM).astype(x.dtype) @ w_o


def setup_inputs(seed: int = 0) -> dict:
    key = jax.random.key(seed)
    ks = jax.random.split(key, 24)
    n_conv = len(range(0, DEPTH, N_MIXERS))
    n_attn = len(range(1, DEPTH, N_MIXERS))
    d, f = D_MODEL, D_FF
    attn_w = HEADS_PER_GROUP * HEAD_DIM

    def nrm(k, shape, scale):
        return scale * jax.random.normal(k, shape, dtype=jnp.float32)

    return {
        "x": nrm(ks[0], (BATCH, SEQ, d), 1.0),
        "ffn1_norm": 1.0 + nrm(ks[1], (DEPTH, d), 0.02),
        "ffn1_wg": nrm(ks[2], (DEPTH, d, f), d ** -0.5),
        "ffn1_wu": nrm(ks[3], (DEPTH, d, f), d ** -0.5),
        "ffn1_wd": nrm(ks[4], (DEPTH, f, d), f ** -0.5),
        "mix_norm": 1.0 + nrm(ks[5], (DEPTH, d), 0.02),
        "ffn2_norm": 1.0 + nrm(ks[6], (DEPTH, d), 0.02),
        "ffn2_wg": nrm(ks[7], (DEPTH, d, f), d ** -0.5),
        "ffn2_wu": nrm(ks[8], (DEPTH, d, f), d ** -0.5),
        "ffn2_wd": nrm(ks[9], (DEPTH, f, d), f ** -0.5),
        "conv_pw1": nrm(ks[10], (n_conv, d, 2 * d), d ** -0.5),
        "conv_pw1_b": nrm(ks[11], (n_conv, 2 * d), 0.02),
        "conv_dw": nrm(ks[12], (n_conv, CONV_WIDTH, d), CONV_WIDTH ** -0.5),
        "conv_dw_b": nrm(ks[13], (n_conv, d), 0.02),
        "conv_ln_g": 1.0 + nrm(ks[14], (n_conv, d), 0.02),
        "conv_ln_b": nrm(ks[15], (n_conv, d), 0.02),
        "conv_pw2": nrm(ks[16], (n_conv, d, d), d ** -0.5),
        "conv_pw2_b": nrm(ks[17], (n_conv, d), 0.02),
        "attn_wqkv": nrm(ks[18], (n_attn, d, QKV_WIDTH), d ** -0.5),
        "attn_wo": nrm(ks[19], (n_attn, attn_w, d), attn_w ** -0.5),
        "final_norm": 1.0 + nrm(ks[20], (d,), 0.02),
    }


def reference(x, ffn1_norm, ffn1_wg, ffn1_wu, ffn1_wd, mix_norm, ffn2_norm, ffn2_wg,
              ffn2_wu, ffn2_wd, conv_pw1, conv_pw1_b, conv_dw, conv_dw_b, conv_ln_g,
              conv_ln_b, conv_pw2, conv_pw2_b, attn_wqkv, attn_wo, final_norm):
    s = x.shape[1]
    cos, sin = rope_tables(s)
    for i in range(DEPTH):
        h = rms_norm(x, ffn1_norm[i])
        x = x + 0.5 * swiglu(h, ffn1_wg[i], ffn1_wu[i], ffn1_wd[i])
        h = rms_norm(x, mix_norm[i])
        j = i // N_MIXERS
        if i % N_MIXERS == 0:
            x = x + conformer_conv(h, conv_pw1[j], conv_pw1_b[j], conv_dw[j], conv_dw_b[j],
                                   conv_ln_g[j], conv_ln_b[j], conv_pw2[j], conv_pw2_b[j])
        else:
            x = x + dilated_attention_mixer(h, attn_wqkv[j], attn_wo[j], cos, sin)
        h = rms_norm(x, ffn2_norm[i])
        x = x + 0.5 * swiglu(h, ffn2_wg[i], ffn2_wu[i], ffn2_wd[i])
    return rms_norm(x, final_norm)
```

```python
from contextlib import ExitStack
import numpy as np
import ml_dtypes
import concourse.bass as bass
import concourse.mybir as mybir
from concourse.bass_utils import run_bass_kernel_spmd

F32 = mybir.dt.float32
BF16 = mybir.dt.bfloat16
ALU = mybir.AluOpType
AF = mybir.ActivationFunctionType
NCORES = 8
RMS_EPS = 1e-6
LN_EPS = 1e-5

FULL = dict(D=4096, F=11008, B=2, S=4096, DEPTH=2)


def _flat(ws):
    out = []
    for w in ws:
        if w is None:
            continue
        if isinstance(w, list):
            out.extend(_flat(w))
        else:
            out.append(w)
    return out


class Sem:
    def __init__(self, h):
        self.h = h
        self.val = 0

    def inc(self, n):
        self.val += n
        return (self, self.val)


class Prog:
    ENG = ("sync", "act", "pool", "dve", "pe")

    def __init__(self, nc, es):
        self.nc = nc
        self.es = es
        self.q = {e: [] for e in self.ENG}
        self.sems = {}
        self.bar = self.S("bar")
        self.nbar = 0
        self.uid = 0

    ALIAS = {}
    for _k, _names in enumerate([
        ("ldg", "ldx0", "ff_wd", "c1_w", "c2_w", "a1_h", "a2_w"),
        ("xcopy", "ldx1", "ff_h", "c1_h", "c2_st0", "a1_wq0", "a2_ld0"),
        ("zinit_d", "ldx2", "ff_w0", "c1_sto", "c2_st1", "a1_wq1", "a2_ld1"),
        ("zinit", "ldx3", "ff_w1", "c1_sts", "c2_cv0", "a1_wv0", "a2_ms"),
        ("init", "ldy0", "dn_st0", "c1_pe", "a1_wv1"),
        ("ldy1", "dn_st1", "c1_act", "a1_sto0"),
        ("rn_sth0", "dn_st2", "c1_dve", "a1_sto1"),
        ("rn_sth1", "dn_st3", "c1_pool", "a1_stv0"),
        ("rn_dvx", "ff_peu", "c1_ms", "c2_pe", "a1_stv1", "a2_pes"),
        ("rn_sq", "ff_act", "c1_sq", "c2_dve", "a1_pe", "a2_pev"),
        ("rn_pe", "ff_dve", "c1_pes", "c2_act", "a1_pesw", "a2_act"),
        ("dn_pe", "c1_evs", "a1_pev"),
        ("rn_dvh", "dn_eva", "a1_act"),
        ("rn_rstd", "dn_evd", "a1_dve"),
        ("c2_cv1", "a1_pool", "a2_dve"),
        ("ldcp", "c2_ms", "a2_pool"),
    ]):
        for _n in _names:
            ALIAS[_n] = f"g{_k}"

    def S(self, name):
        name = self.ALIAS.get(name, name)
        if name not in self.sems:
            self.sems[name] = Sem(self.es.enter_context(self.nc.semaphore(name)))
        return self.sems[name]

    def op(self, eng, fn, waits=(), inc=None):
        tok = None
        if inc is not None:
            tok = inc[0].inc(inc[1])
        self.q[eng].append((tuple(_flat(waits)), fn, inc))
        return tok

    def wait(self, eng, waits):
        self.q[eng].append((tuple(_flat(waits)), None, None))

    def emit(self, e, name):
        for waits, fn, inc in self.q[name]:
            for s, v in waits:
                e.wait_ge(s.h, v)
            if fn is None:
                continue
            ins = fn(e)
            if inc is not None:
                ins.then_inc(inc[0].h, inc[1])

    def flush(self):
        nc = self.nc
        P = self
        with nc.Block() as block:
            @block.sync
            def _(e):
                P.emit(e, "sync")

            @block.scalar
            def _(e):
                P.emit(e, "act")

            @block.gpsimd
            def _(e):
                P.emit(e, "pool")

            @block.vector
            def _(e):
                P.emit(e, "dve")

            @block.tensor
            def _(e):
                P.emit(e, "pe")
        self.q = {e: [] for e in self.ENG}

    def barrier(self, scr, extra=()):
        self.nbar += 1
        self.op("act", lambda e: e.activation(out=scr[:, 0:1], in_=scr[:, 0:1], func=AF.Copy),
                inc=(self.bar, 1))
        self.op("dve", lambda e: e.memset(scr[:, 1:2], 0.0), inc=(self.bar, 1))
        self.op("pool", lambda e: e.memset(scr[:, 2:3], 0.0), inc=(self.bar, 1))
        tok = (self.bar, self.bar.val)
        for eng in ("sync", "act", "pool", "dve", "pe"):
            self.wait(eng, [tok] + list(extra))
        return tok


def phase_rn(P, es0, cfg, x_res, y_rs, scale, bias_sb, gain_sb, h_out, final_out, scr, ones_f, ready=None, eps_sb=None):
    nc = P.nc
    D, T = cfg["D"], cfg["T"]
    NCH = D // 128
    NT = T // 512
    P.uid += 1
    U = f"_{P.uid}"
    with ExitStack() as es:
        xs = es.enter_context(nc.sbuf_tensor("rn_xs" + U, [128, NCH, T], F32))
        ys = [es.enter_context(nc.sbuf_tensor(f"rn_ys{i}" + U, [128, T], F32)) for i in range(2)]
        sq = [es.enter_context(nc.sbuf_tensor(f"rn_sq{i}" + U, [128, T], F32)) for i in range(2)]
        rstd = es.enter_context(nc.sbuf_tensor("rn_rstd" + U, [128, T], F32))
        odt = F32 if final_out is not None else BF16
        hs = [es.enter_context(nc.sbuf_tensor(f"rn_hs{i}" + U, [128, T], odt)) for i in range(2)]
        ss = [es.enter_context(nc.psum_tensor(f"rn_ss{i}" + U, [128, 512], F32)) for i in range(NT)]
        s_ldx = [P.S(f"ldx{c % 4}") for c in range(NCH)]
        tok_xuse = [None] * NCH
        s_ldy = [P.S(f"ldy{i}") for i in range(2)]
        s_dvx = P.S("rn_dvx")
        s_sq = P.S("rn_sq")
        s_pe = P.S("rn_pe")
        s_stx = P.S("rn_stx")
        s_sth = [P.S(f"rn_sth{i}") for i in range(2)]
        s_dvh = P.S("rn_dvh")
        s_rstd = P.S("rn_rstd")
        xr = x_res.ap().rearrange("(c p) t -> c p t", p=128)
        tok_dvx = []
        tok_sq = [None] * NCH
        tok_pe = [None] * NCH
        tok_yfree = [None, None]
        tok_stx = None
        for c in range(NCH):
            t_ldx = P.op("sync", lambda e, c=c: e.dma_start(out=xs[:, c, :], in_=xr[c]),
                         waits=[ready, tok_xuse[c - 4] if c >= 4 else None], inc=(s_ldx[c], 16))
            if y_rs is not None:
                yr = y_rs.ap().rearrange("(c p) t -> c p t", p=128)
                i = c % 2
                t_ldy = P.op("sync", lambda e, c=c, i=i: e.dma_start(out=ys[i][:], in_=yr[c]),
                             waits=[tok_yfree[i], ready], inc=(s_ldy[i], 16))
                t = P.op("dve", lambda e, c=c, i=i: e.scalar_tensor_tensor(
                    out=xs[:, c, :], in0=ys[i][:], scalar=float(scale), in1=xs[:, c, :],
                    op0=ALU.mult, op1=ALU.add), waits=[t_ldx, t_ldy], inc=(s_dvx, 1))
                if bias_sb is not None:
                    t = P.op("dve", lambda e, c=c: e.tensor_scalar(
                        out=xs[:, c, :], in0=xs[:, c, :], scalar1=ones_f[:, 0:1], scalar2=bias_sb[:, c:c + 1],
                        op0=ALU.mult, op1=ALU.add), waits=[t], inc=(s_dvx, 1))
                tok_yfree[i] = t
                tok_stx = P.op("pool", lambda e, c=c: e.dma_start(out=xr[c], in_=xs[:, c, :]),
                               waits=[t], inc=(s_stx, 16))
                t_x = t
            else:
                t_x = t_ldx
            i = c % 2
            tok_xuse[c] = t_x if y_rs is not None else None
            tok_sq[c] = P.op("act", lambda e, c=c, i=i: e.activation(out=sq[i][:], in_=xs[:, c, :], func=AF.Square),
                             waits=[t_x, tok_pe[c - 2] if c >= 2 else None], inc=(s_sq, 1))
            if tok_xuse[c] is None:
                tok_xuse[c] = tok_sq[c]
            for n in range(NT):
                tk = P.op("pe", lambda e, c=c, i=i, n=n: e.matmul(
                    ss[n][:], lhsT=ones_f[:], rhs=sq[i][:, n * 512:(n + 1) * 512],
                    start=(c == 0), stop=(c == NCH - 1)),
                    waits=[tok_sq[c]] if n == 0 else [], inc=(s_pe, 1))
            tok_pe[c] = tk
        for n in range(NT):
            t_r0 = P.op("act", lambda e, n=n: e.activation(
                out=rstd[:, n * 512:(n + 1) * 512], in_=ss[n][:], func=AF.Sqrt, bias=eps_sb[:, 0:1], scale=1.0 / D),
                waits=[tok_pe[NCH - 1]] if n == 0 else [], inc=(s_rstd, 1))
        t_r = P.op("dve", lambda e: e.reciprocal(out=rstd[:], in_=rstd[:]), waits=[t_r0], inc=(s_rstd, 1))
        dst = final_out if final_out is not None else h_out
        dr = dst.ap().rearrange("(c p) t -> c p t", p=128)
        tok_hfree = [None, None]
        toks = []
        for c in range(NCH):
            i = c % 2
            t = P.op("dve", lambda e, c=c, i=i: e.scalar_tensor_tensor(
                out=hs[i][:], in0=xs[:, c, :], scalar=gain_sb[:, c:c + 1], in1=rstd[:],
                op0=ALU.mult, op1=ALU.mult), waits=[tok_hfree[i], t_r], inc=(s_dvh, 1))
            tok_hfree[i] = P.op("sync", lambda e, c=c, i=i: e.dma_start(out=dr[c], in_=hs[i][:]),
                                waits=[t], inc=(s_sth[i], 16))
        done = [tok_hfree[0], tok_hfree[1], tok_stx]
        P.barrier(scr, done)
        P.flush()
        return done


def coll(P, kind, src, dst, waits, ncores):
    op = ALU.add if kind == "ReduceScatter" else ALU.bypass
    s = P.S("cc")
    return P.op("pool", lambda e: e.collective_compute(
        kind, op, replica_groups=[list(range(ncores))], ins=[src], outs=[dst]),
        waits=waits, inc=(s, 1))


def cast_dram(P, src, dst, rows, cols, sem):
    step = max(1, (8 << 20) // (cols * 4))
    tok = None
    r = 0
    while r < rows:
        n = min(step, rows - r)
        tok = P.op("pool", lambda e, r=r, n=n: e.dma_start(out=dst[r:r + n, :], in_=src[r:r + n, :]),
                   inc=(sem, 16))
        r += n
    return tok


class Down:
    def __init__(self, P, es, U, NCH, tag):
        nc = P.nc
        self.P = P
        self.NCH = NCH
        self.py = [es.enter_context(nc.psum_tensor(f"{tag}_py{i}" + U, [128, 512], F32)) for i in range(2)]
        self.yst = [es.enter_context(nc.sbuf_tensor(f"{tag}_yst{i}" + U, [128, 512], F32)) for i in range(4)]
        self.s_pe = P.S("dn_pe")
        self.s_ev = [P.S("dn_eva"), P.S("dn_evd")]
        self.s_st = [P.S(f"dn_st{i}") for i in range(4)]
        self.pyfree = [None, None]
        self.ystfree = [None] * 4
        self.m = 0

    def run(self, wds, zT, NK, ydst, first_waits, zsl=None):
        P = self.P
        py, yst = self.py, self.yst
        if zsl is None:
            zsl = slice(0, 512)
        t_pe = None
        for ic in range(self.NCH):
            i = self.m % 2
            for k in range(NK):
                w = []
                if k == 0:
                    w = [self.pyfree[i]]
                    if ic == 0:
                        w += list(first_waits)
                t_pe = P.op("pe", lambda e, k=k, ic=ic, i=i: e.matmul(
                    py[i][:], lhsT=wds[:, k, ic * 128:(ic + 1) * 128], rhs=zT[:, k, zsl],
                    start=(k == 0), stop=(k == NK - 1)), waits=w,
                    inc=(self.s_pe, 1) if k == NK - 1 else None)
            q = self.m % 4
            if i == 0:
                t_e = P.op("act", lambda e, i=i, q=q: e.activation(out=yst[q][:], in_=py[i][:], func=AF.Copy),
                           waits=[t_pe, self.ystfree[q]], inc=(self.s_ev[0], 1))
            else:
                t_e = P.op("dve", lambda e, i=i, q=q: e.tensor_copy(out=yst[q][:], in_=py[i][:]),
                           waits=[t_pe, self.ystfree[q]], inc=(self.s_ev[1], 1))
            self.pyfree[i] = t_e
            dst = ydst(ic)
            self.ystfree[q] = P.op("sync", lambda e, dst=dst, q=q: e.dma_start(out=dst, in_=yst[q][:]),
                                   waits=[t_e], inc=(self.s_st[q], 16))
            self.m += 1
        return t_pe

    def done(self):
        return [t for t in self.ystfree if t is not None]


def tile_of(cfg, n0):
    T = cfg["T"]
    return n0 // T, n0 % T


def phase_ffn(P, cfg, h_all, wg, wu, wd, y_part, scr, ready):
    nc = P.nc
    D, T, NJ, NC = cfg["D"], cfg["T"], cfg["NJ"], cfg["NC"]
    NCH = D // 128
    NTOK = NC * T
    P.uid += 1
    U = f"_{P.uid}"
    with ExitStack() as es:
        hT = es.enter_context(nc.sbuf_tensor("ff_hT" + U, [128, NCH, 512], BF16))
        aT = es.enter_context(nc.sbuf_tensor("ff_aT" + U, [128, NJ, 512], BF16))
        wds = es.enter_context(nc.sbuf_tensor("ff_wd" + U, [128, NJ, D], BF16))
        wgs = [es.enter_context(nc.sbuf_tensor(f"ff_wg{i}" + U, [128, NCH, 128], BF16)) for i in range(2)]
        wus = [es.enter_context(nc.sbuf_tensor(f"ff_wu{i}" + U, [128, NCH, 128], BF16)) for i in range(2)]
        sg = [es.enter_context(nc.sbuf_tensor(f"ff_sg{i}" + U, [128, 512], F32)) for i in range(2)]
        pg = [es.enter_context(nc.psum_tensor(f"ff_pg{i}" + U, [128, 512], F32)) for i in range(2)]
        pu = [es.enter_context(nc.psum_tensor(f"ff_pu{i}" + U, [128, 512], F32)) for i in range(2)]
        dn = Down(P, es, U, NCH, "ff")
        s_wd = P.S("ff_wd")
        s_h = P.S("ff_h")
        s_w = [P.S(f"ff_w{i}") for i in range(2)]
        s_peu = P.S("ff_peu")
        s_act = P.S("ff_act")
        s_dve = P.S("ff_dve")
        t_wd = P.op("sync", lambda e: e.dma_start(out=wds[:], in_=wd.ap().rearrange("(j p) d -> p j d", p=128)),
                    waits=ready, inc=(s_wd, 16))
        hr = h_all.ap().rearrange("(r c p) t -> r p c t", p=128, c=NCH)
        yr = y_part.ap().rearrange("(r c p) t -> r c p t", p=128, c=NCH)
        wgr = wg.ap().rearrange("(j p) (c f) -> j p c f", p=128, f=128)
        wur = wu.ap().rearrange("(j p) (c f) -> j p c f", p=128, f=128)
        tok_hfree = None
        tok_wfree = [None, None]
        tok_pgfree = [None, None]
        tok_afree = None
        k = 0
        for n0 in range(0, NTOK, 512):
            r, c0 = tile_of(cfg, n0)
            t_h = P.op("sync", lambda e, r=r, c0=c0: e.dma_start(out=hT[:], in_=hr[r][:, :, c0:c0 + 512]),
                       waits=list(ready) + [tok_hfree], inc=(s_h, 16))
            tok_a = None
            for j in range(NJ):
                i = k % 2
                P.op("sync", lambda e, j=j, i=i: e.dma_start(out=wgs[i][:], in_=wgr[j]),
                     waits=list(ready) + [tok_wfree[i]], inc=(s_w[i], 16))
                t_w = P.op("sync", lambda e, j=j, i=i: e.dma_start(out=wus[i][:], in_=wur[j]),
                           inc=(s_w[i], 16))
                for c in range(NCH):
                    P.op("pe", lambda e, c=c, i=i: e.matmul(pg[i][:], lhsT=wgs[i][:, c, :], rhs=hT[:, c, :],
                                                          start=(c == 0), stop=(c == NCH - 1)),
                         waits=[t_w, t_h, tok_pgfree[i]] if c == 0 else [])
                for c in range(NCH):
                    t_pe = P.op("pe", lambda e, c=c, i=i: e.matmul(pu[i][:], lhsT=wus[i][:, c, :], rhs=hT[:, c, :],
                                                                 start=(c == 0), stop=(c == NCH - 1)),
                                inc=(s_peu, 1) if c == NCH - 1 else None)
                tok_wfree[i] = t_pe
                t_s = P.op("act", lambda e, i=i: e.activation(out=sg[i][:], in_=pg[i][:], func=AF.Silu),
                           waits=[t_pe, tok_pgfree[i]], inc=(s_act, 1))
                t_a = P.op("dve", lambda e, i=i, j=j: e.tensor_tensor(out=aT[:, j, :], in0=sg[i][:], in1=pu[i][:],
                                                                   op=ALU.mult),
                           waits=[t_s, tok_afree] if j == 0 else [t_s], inc=(s_dve, 1))
                tok_pgfree[i] = t_a
                tok_a = t_a
                k += 1
            tok_hfree = t_pe
            tok_afree = dn.run(wds, aT, NJ, lambda ic, r=r, c0=c0: yr[r][ic][:, c0:c0 + 512], [tok_a, t_wd])
        done = dn.done()
        P.barrier(scr, done)
        P.flush()
        return done


def phase_conv1(P, cfg, h_all, pw1, cvp, conv_out, stats_own, scr, ones_f, ready):
    nc = P.nc
    D, T, NC, S, B = cfg["D"], cfg["T"], cfg["NC"], cfg["S"], cfg["B"]
    NCH = D // 128
    NK = D // NC // 128
    NTOK = NC * T
    W = 31
    PW = min(1024, S)
    NPH = PW // 512
    P.uid += 1
    U = f"_{P.uid}"
    with ExitStack() as es:
        hT = es.enter_context(nc.sbuf_tensor("c1_hT" + U, [128, NCH, 512], BF16))
        w1 = es.enter_context(nc.sbuf_tensor("c1_w1" + U, [128, 2 * NK, NCH, 128], BF16))
        glu = es.enter_context(nc.sbuf_tensor("c1_glu" + U, [128, NK, S + 30], F32))
        acc = [es.enter_context(nc.sbuf_tensor(f"c1_acc{k}" + U, [128, PW], F32)) for k in range(NK)]
        sb = [es.enter_context(nc.sbuf_tensor(f"c1_sb{i}" + U, [128, 512], F32)) for i in range(2)]
        sq = es.enter_context(nc.sbuf_tensor("c1_sq" + U, [128, PW], F32))
        stt = es.enter_context(nc.sbuf_tensor("c1_stt" + U, [1, 2, PW], F32))
        pa = [es.enter_context(nc.psum_tensor(f"c1_pa{i}" + U, [128, 512], F32)) for i in range(2)]
        pb = [es.enter_context(nc.psum_tensor(f"c1_pb{i}" + U, [128, 512], F32)) for i in range(2)]
        ps1 = [es.enter_context(nc.psum_tensor(f"c1_ps1{i}" + U, [128, 512], F32)) for i in range(NPH)]
        ps2 = [es.enter_context(nc.psum_tensor(f"c1_ps2{i}" + U, [128, 512], F32)) for i in range(NPH)]
        s_w = P.S("c1_w")
        s_h = P.S("c1_h")
        s_pe = P.S("c1_pe")
        s_act = P.S("c1_act")
        s_dve = P.S("c1_dve")
        s_pool = P.S("c1_pool")
        s_ms = P.S("c1_ms")
        s_sq = P.S("c1_sq")
        s_pes = P.S("c1_pes")
        s_evs = P.S("c1_evs")
        s_sto = P.S("c1_sto")
        s_sts = P.S("c1_sts")
        t_w = P.op("sync", lambda e: e.dma_start(
            out=w1[:], in_=pw1.ap().rearrange("(j p) (c f) -> p j c f", p=128, f=128)), waits=ready, inc=(s_w, 16))
        for k in range(NK):
            P.op("dve", lambda e, k=k: e.memset(glu[:, k, 0:15], 0.0), inc=(s_ms, 1))
            t_ms = P.op("dve", lambda e, k=k: e.memset(glu[:, k, S + 15:S + 30], 0.0), inc=(s_ms, 1))
        hr = h_all.ap().rearrange("(r c p) t -> r p c t", p=128, c=NCH)
        co = conv_out.ap().rearrange("(k p) n -> k p n", p=128)
        so = stats_own.ap().rearrange("(o a) c -> o (a c)", o=1)[:, 0:2 * NTOK]
        tok_hfree = None
        tok_pfree = [None, None]
        tok_sbfree = [None, None]
        tok_glufree = None
        tok_accfree = [None] * NK
        tok_sqfree = None
        tok_sttfree = None
        tok_psfree = None
        g = 0
        unit = 0
        stores = []
        for b in range(B):
            last_glu = None
            for s0 in range(0, S, 512):
                n0 = b * S + s0
                r, c0 = tile_of(cfg, n0)
                t_h = P.op("sync", lambda e, r=r, c0=c0: e.dma_start(out=hT[:], in_=hr[r][:, :, c0:c0 + 512]),
                           waits=list(ready) + [tok_hfree], inc=(s_h, 16))
                for k in range(NK):
                    i = g % 2
                    for c in range(NCH):
                        P.op("pe", lambda e, c=c, i=i, k=k: e.matmul(pa[i][:], lhsT=w1[:, k, c, :], rhs=hT[:, c, :],
                                                                   start=(c == 0), stop=(c == NCH - 1)),
                             waits=[t_w, t_h, tok_pfree[i]] if c == 0 else [])
                    for c in range(NCH):
                        t_pe = P.op("pe", lambda e, c=c, i=i, k=k: e.matmul(pb[i][:], lhsT=w1[:, NK + k, c, :],
                                                                          rhs=hT[:, c, :],
                                                                          start=(c == 0), stop=(c == NCH - 1)),
                                    inc=(s_pe, 1) if c == NCH - 1 else None)
                    t_s = P.op("act", lambda e, i=i, k=k: e.activation(out=sb[i][:], in_=pb[i][:], func=AF.Sigmoid,
                                                                     bias=cvp[:, k, 1:2], scale=1.0),
                               waits=[t_pe, tok_sbfree[i]], inc=(s_act, 1))
                    t_g = P.op("dve", lambda e, i=i, k=k, s0=s0: e.scalar_tensor_tensor(
                        out=glu[:, k, 15 + s0:15 + s0 + 512], in0=pa[i][:], scalar=cvp[:, k, 0:1], in1=sb[i][:],
                        op0=ALU.add, op1=ALU.mult), waits=[t_s, tok_glufree, t_ms], inc=(s_dve, 1))
                    tok_pfree[i] = t_g
                    tok_sbfree[i] = t_g
                    last_glu = t_g
                    g += 1
                tok_hfree = t_pe
            last_conv = []
            for s1 in range(0, S, PW):
                n1 = b * S + s1
                t_units = []
                for k in range(NK):
                    eng = "dve"
                    sem = s_pool if eng == "pool" else s_dve
                    unit += 1
                    t_c = P.op(eng, lambda e, k=k, s1=s1: e.tensor_scalar(
                        out=acc[k][:], in0=glu[:, k, s1:s1 + PW], scalar1=cvp[:, k, 3:4], scalar2=cvp[:, k, 2:3],
                        op0=ALU.mult, op1=ALU.add), waits=[last_glu, tok_accfree[k]], inc=(sem, 1))
                    for w in range(1, W):
                        t_c = P.op(eng, lambda e, k=k, s1=s1, w=w: e.scalar_tensor_tensor(
                            out=acc[k][:], in0=glu[:, k, s1 + w:s1 + w + PW], scalar=cvp[:, k, 3 + w:4 + w],
                            in1=acc[k][:], op0=ALU.mult, op1=ALU.add), waits=[t_c], inc=(sem, 1))
                    t_units.append(t_c)
                    last_conv.append(t_c)
                    t_st = P.op("sync", lambda e, k=k, n1=n1: e.dma_start(out=co[k][:, n1:n1 + PW], in_=acc[k][:]),
                                waits=[t_c], inc=(s_sto, 16))
                    stores.append(t_st)
                    t_q = P.op("act", lambda e, k=k: e.activation(out=sq[:], in_=acc[k][:], func=AF.Square),
                               waits=[t_c, tok_sqfree], inc=(s_sq, 1))
                    for hf in range(NPH):
                        P.op("pe", lambda e, k=k, hf=hf: e.matmul(ps1[hf][:], lhsT=ones_f[:],
                                                                rhs=acc[k][:, hf * 512:(hf + 1) * 512],
                                                                start=(k == 0), stop=(k == NK - 1)),
                             waits=[t_q, tok_psfree] if hf == 0 and k == 0 else ([t_q] if hf == 0 else []))
                        t_ps = P.op("pe", lambda e, k=k, hf=hf: e.matmul(ps2[hf][:], lhsT=ones_f[:],
                                                                       rhs=sq[:, hf * 512:(hf + 1) * 512],
                                                                       start=(k == 0), stop=(k == NK - 1)),
                                    inc=(s_pes, 1) if hf == NPH - 1 else None)
                    tok_sqfree = t_ps
                    tok_accfree[k] = [t_ps, t_st]
                for hf in range(NPH):
                    P.op("act", lambda e, hf=hf: e.activation(out=stt[0:1, 0, hf * 512:(hf + 1) * 512],
                                                             in_=ps1[hf][0:1, :], func=AF.Copy),
                         waits=[t_ps, tok_sttfree] if hf == 0 else [], inc=(s_evs, 1))
                    t_ev = P.op("act", lambda e, hf=hf: e.activation(out=stt[0:1, 1, hf * 512:(hf + 1) * 512],
                                                                    in_=ps2[hf][0:1, :], func=AF.Copy),
                                inc=(s_evs, 1))
                tok_psfree = t_ev
                P.op("sync", lambda e, n1=n1: e.dma_start(out=so[0:1, n1:n1 + PW], in_=stt[0:1, 0, :]),
                     waits=[t_ev], inc=(s_sts, 16))
                tok_sttfree = P.op("sync", lambda e, n1=n1: e.dma_start(
                    out=so[0:1, NTOK + n1:NTOK + n1 + PW], in_=stt[0:1, 1, :]), inc=(s_sts, 16))
                stores.append(tok_sttfree)
            tok_glufree = last_conv
        done = stores
        P.barrier(scr, done)
        P.flush()
        return done


def phase_conv2(P, cfg, conv_out, stats_all, lnp, pw2, y_part, scr, ones_f, eps_sb, ready):
    nc = P.nc
    D, T, NC = cfg["D"], cfg["T"], cfg["NC"]
    NCH = D // 128
    NK = D // NC // 128
    NTOK = NC * T
    P.uid += 1
    U = f"_{P.uid}"
    with ExitStack() as es:
        w2 = es.enter_context(nc.sbuf_tensor("c2_w2" + U, [128, NK, D], BF16))
        zT = es.enter_context(nc.sbuf_tensor("c2_zT" + U, [128, NK, 512], BF16))
        cv = [es.enter_context(nc.sbuf_tensor(f"c2_cv{i}" + U, [128, 512], F32)) for i in range(2)]
        st8 = [es.enter_context(nc.sbuf_tensor(f"c2_st{i}" + U, [128, 2, 512], F32)) for i in range(2)]
        mu = es.enter_context(nc.sbuf_tensor("c2_mu" + U, [128, 512], F32))
        var = es.enter_context(nc.sbuf_tensor("c2_var" + U, [128, 512], F32))
        rstd = es.enter_context(nc.sbuf_tensor("c2_rstd" + U, [128, 512], F32))
        pm1 = es.enter_context(nc.psum_tensor("c2_pm1" + U, [128, 512], F32))
        pm2 = es.enter_context(nc.psum_tensor("c2_pm2" + U, [128, 512], F32))
        dn = Down(P, es, U, NCH, "c2")
        s_w = P.S("c2_w")
        s_st = [P.S(f"c2_st{i}") for i in range(2)]
        s_cv = [P.S(f"c2_cv{i}") for i in range(2)]
        s_pe = P.S("c2_pe")
        s_dve = P.S("c2_dve")
        s_act = P.S("c2_act")
        t_w = P.op("sync", lambda e: e.dma_start(out=w2[:], in_=pw2.ap().rearrange("(j p) d -> p j d", p=128)),
                   waits=ready, inc=(s_w, 16))
        s_ms = P.S("c2_ms")
        P.op("pool", lambda e: e.memset(st8[0][:], 0.0), inc=(s_ms, 1))
        t_ms = P.op("pool", lambda e: e.memset(st8[1][:], 0.0), inc=(s_ms, 1))
        yr = y_part.ap().rearrange("(r c p) t -> r c p t", p=128, c=NCH)
        co = conv_out.ap().rearrange("(k p) n -> k p n", p=128)
        sa = stats_all.ap().rearrange("(r a) c -> r (a c)", a=128)[:, 0:2 * NTOK].rearrange("r (s n) -> r s n", s=2)
        tok_stfree = [None, None]
        tok_cvfree = [None, None]
        tok_pmfree = None
        tok_zfree = None
        tok_stat_free = None
        ti = 0
        ci = 0
        for n0 in range(0, NTOK, 512):
            r, c0 = tile_of(cfg, n0)
            i = ti % 2
            t_st = P.op("sync", lambda e, i=i, n0=n0: e.dma_start(out=st8[i][0:NC, :, :], in_=sa[:, :, n0:n0 + 512]),
                        waits=list(ready) + [tok_stfree[i], t_ms], inc=(s_st[i], 16))
            P.op("pe", lambda e, i=i: e.matmul(pm1[:], lhsT=ones_f[:], rhs=st8[i][:, 0, :], start=True, stop=True),
                 waits=[t_st, tok_pmfree])
            t_pm = P.op("pe", lambda e, i=i: e.matmul(pm2[:], lhsT=ones_f[:], rhs=st8[i][:, 1, :],
                                                    start=True, stop=True), inc=(s_pe, 1))
            tok_stfree[i] = t_pm
            t1 = P.op("dve", lambda e: e.tensor_scalar(out=mu[:], in0=pm1[:], scalar1=1.0 / D, scalar2=0.0,
                                                       op0=ALU.mult, op1=ALU.add), waits=[t_pm, tok_stat_free], inc=(s_dve, 1))
            t2 = P.op("dve", lambda e: e.tensor_tensor(out=var[:], in0=mu[:], in1=mu[:], op=ALU.mult),
                      waits=[t1], inc=(s_dve, 1))
            t3 = P.op("dve", lambda e: e.scalar_tensor_tensor(out=var[:], in0=pm2[:], scalar=1.0 / D, in1=var[:],
                                                              op0=ALU.mult, op1=ALU.subtract),
                      waits=[t2], inc=(s_dve, 1))
            tok_pmfree = t3
            t4 = P.op("act", lambda e: e.activation(out=rstd[:], in_=var[:], func=AF.Sqrt, bias=eps_sb[:, 1:2],
                                                    scale=1.0), waits=[t3], inc=(s_act, 1))
            t5 = P.op("dve", lambda e: e.reciprocal(out=rstd[:], in_=rstd[:]), waits=[t4], inc=(s_dve, 1))
            t_z = None
            DBG2 = cfg.get("DBG2", 9)
            if DBG2 <= 1:
                tok_stat_free = t5
                ti += 1
                continue
            for k in range(NK):
                j = ci % 2
                t_cv = P.op("sync", lambda e, j=j, k=k, n0=n0: e.dma_start(out=cv[j][:], in_=co[k][:, n0:n0 + 512]),
                            waits=list(ready) + [tok_cvfree[j]], inc=(s_cv[j], 16))
                if DBG2 == 2.1:
                    tok_cvfree[j] = t_cv
                    ci += 1
                    continue
                t6 = P.op("dve", lambda e, j=j: e.tensor_tensor(out=cv[j][:], in0=cv[j][:], in1=mu[:],
                                                              op=ALU.subtract), waits=[t_cv, t5], inc=(s_dve, 1))
                if DBG2 == 2.2:
                    tok_cvfree[j] = t6
                    tok_stat_free = t6
                    ci += 1
                    continue
                t7 = P.op("dve", lambda e, j=j: e.tensor_tensor(out=cv[j][:], in0=cv[j][:], in1=rstd[:], op=ALU.mult),
                          waits=[t6], inc=(s_dve, 1))
                t7 = P.op("dve", lambda e, j=j, k=k: e.tensor_scalar(out=cv[j][:], in0=cv[j][:], scalar1=lnp[:, k, 0:1],
                                                                   scalar2=lnp[:, k, 1:2], op0=ALU.mult, op1=ALU.add),
                          waits=[t7], inc=(s_dve, 1))
                t_z = P.op("act", lambda e, j=j, k=k: e.activation(out=zT[:, k, :], in_=cv[j][:], func=AF.Silu),
                           waits=[t7, tok_zfree] if k == 0 else [t7], inc=(s_act, 1))
                tok_cvfree[j] = t_z
                tok_stat_free = t7
                ci += 1
            if DBG2 <= 2:
                tok_zfree = None
                ti += 1
                continue
            tok_zfree = dn.run(w2, zT, NK, lambda ic, r=r, c0=c0: yr[r][ic][:, c0:c0 + 512], [t_z, t_w])
            ti += 1
        done = dn.done()
        P.barrier(scr, done)
        P.flush()
        return done


GROUPS = ((128, 1), (512, 4), (2048, 16))


def phase_attn1(P, cfg, h_all, wqk, wv, rope_sb, perm_sb, qk_d, v_d, scr, ready):
    nc = P.nc
    D, T, NC, S, B = cfg["D"], cfg["T"], cfg["NC"], cfg["S"], cfg["B"]
    NCH = D // 128
    NHL = D // 128 // NC
    NQK = 3 * 2 * NHL
    NTOK = NC * T
    VW = NHL * 128
    P.uid += 1
    U = f"_{P.uid}"
    with ExitStack() as es:
        hT = es.enter_context(nc.sbuf_tensor("a1_hT" + U, [128, NCH, 512], BF16))
        wq = [es.enter_context(nc.sbuf_tensor(f"a1_wq{i}" + U, [128, NCH, 128], BF16)) for i in range(2)]
        wvs = [es.enter_context(nc.sbuf_tensor(f"a1_wv{i}" + U, [128, NCH, VW], BF16)) for i in range(2)]
        tq = [es.enter_context(nc.sbuf_tensor(f"a1_tq{i}" + U, [128, 512], BF16)) for i in range(2)]
        r1 = [es.enter_context(nc.sbuf_tensor(f"a1_r1{i}" + U, [128, 512], F32)) for i in range(2)]
        r2 = [es.enter_context(nc.sbuf_tensor(f"a1_r2{i}" + U, [128, 512], F32)) for i in range(2)]
        ro = [es.enter_context(nc.sbuf_tensor(f"a1_ro{i}" + U, [128, 512], BF16)) for i in range(2)]
        vo = [es.enter_context(nc.sbuf_tensor(f"a1_vo{i}" + U, [128, VW], BF16)) for i in range(2)]
        pq = [es.enter_context(nc.psum_tensor(f"a1_pq{i}" + U, [128, 512], F32)) for i in range(2)]
        psw = [es.enter_context(nc.psum_tensor(f"a1_psw{i}" + U, [128, 512], F32)) for i in range(2)]
        pv = [es.enter_context(nc.psum_tensor(f"a1_pv{i}" + U, [128, 512], F32)) for i in range(2)]
        s_h = P.S("a1_h")
        s_wq = [P.S(f"a1_wq{i}") for i in range(2)]
        s_wv = [P.S(f"a1_wv{i}") for i in range(2)]
        s_pe = P.S("a1_pe")
        s_pesw = P.S("a1_pesw")
        s_pev = P.S("a1_pev")
        s_act = P.S("a1_act")
        s_dve = P.S("a1_dve")
        s_pool = P.S("a1_pool")
        s_sto = [P.S(f"a1_sto{i}") for i in range(2)]
        s_stv = [P.S(f"a1_stv{i}") for i in range(2)]
        hr = h_all.ap().rearrange("(r c p) t -> r p c t", p=128, c=NCH)
        wqr = wqk.ap().rearrange("(j p) (c f) -> j p c f", p=128, f=128)
        wvr = wv.ap().rearrange("(c p) (g f) -> g p c f", p=128, f=VW)
        qkr = qk_d.ap().rearrange("(x p) s -> x p s", p=128)
        vr = v_d.ap().rearrange("(x s) f -> x s f", s=S)
        tok_hfree = None
        tok_wqfree = [None, None]
        tok_wvfree = [None, None]
        tok_pqfree = [None, None]
        tok_tqfree = [None, None]
        tok_pswfree = [None, None]
        tok_r1free = [None, None]
        tok_r2free = [None, None]
        tok_rofree = [None, None]
        tok_pvfree = [None, None]
        tok_vofree = [None, None]
        kq = 0
        kv = 0
        kvo = 0
        stores = []
        for n0 in range(0, NTOK, 512):
            r, c0 = tile_of(cfg, n0)
            b, s0 = n0 // S, n0 % S
            t_h = P.op("sync", lambda e, r=r, c0=c0: e.dma_start(out=hT[:], in_=hr[r][:, :, c0:c0 + 512]),
                       waits=list(ready) + [tok_hfree], inc=(s_h, 16))
            last_pe = None
            for j in range(NQK):
                i = kq % 2
                g, rem = divmod(j, 2 * NHL)
                w_, hl = divmod(rem, NHL)
                t_w = P.op("sync", lambda e, j=j, i=i: e.dma_start(out=wq[i][:], in_=wqr[j]),
                           waits=list(ready) + [tok_wqfree[i]], inc=(s_wq[i], 16))
                for c in range(NCH):
                    t_pe = P.op("pe", lambda e, c=c, i=i: e.matmul(pq[i][:], lhsT=wq[i][:, c, :], rhs=hT[:, c, :],
                                                                 start=(c == 0), stop=(c == NCH - 1)),
                                waits=[t_w, t_h, tok_pqfree[i]] if c == 0 else [],
                                inc=(s_pe, 1) if c == NCH - 1 else None)
                tok_wqfree[i] = t_pe
                last_pe = t_pe
                t_tq = P.op("act", lambda e, i=i: e.activation(out=tq[i][:], in_=pq[i][:], func=AF.Copy),
                            waits=[t_pe, tok_tqfree[i]], inc=(s_act, 1))
                t_sw = P.op("pe", lambda e, i=i: e.matmul(psw[i][:], lhsT=perm_sb[:], rhs=tq[i][:],
                                                        start=True, stop=True),
                            waits=[t_tq, tok_pswfree[i]], inc=(s_pesw, 1))
                tok_tqfree[i] = t_sw
                t_r1 = P.op("dve", lambda e, i=i, s0=s0: e.tensor_tensor(out=r1[i][:], in0=pq[i][:],
                                                                       in1=rope_sb[:, 0, s0:s0 + 512], op=ALU.mult),
                            waits=[t_pe, t_tq, tok_r1free[i]], inc=(s_dve, 1))
                tok_pqfree[i] = [t_r1, t_tq]
                t_r2 = P.op("dve", lambda e, i=i, s0=s0: e.tensor_tensor(out=r2[i][:], in0=psw[i][:],
                                                                       in1=rope_sb[:, 1, s0:s0 + 512], op=ALU.mult),
                            waits=[t_sw, tok_r2free[i]], inc=(s_dve, 1))
                tok_pswfree[i] = t_r2
                t_ro = P.op("pool", lambda e, i=i: e.tensor_tensor(out=ro[i][:], in0=r1[i][:], in1=r2[i][:], op=ALU.add),
                            waits=[t_r1, t_r2, tok_rofree[i]], inc=(s_pool, 1))
                tok_r1free[i] = t_ro
                tok_r2free[i] = t_ro
                x = ((b * 3 + g) * 2 + w_) * NHL + hl
                tok_rofree[i] = P.op("sync", lambda e, i=i, x=x, s0=s0: e.dma_start(
                    out=qkr[x][:, s0:s0 + 512], in_=ro[i][:]), waits=[t_ro], inc=(s_sto[i], 16))
                kq += 1
            for g in range(3):
                i = kv % 2
                t_wv = P.op("sync", lambda e, g=g, i=i: e.dma_start(out=wvs[i][:], in_=wvr[g]),
                            waits=list(ready) + [tok_wvfree[i]], inc=(s_wv[i], 16))
                for ts in range(4):
                    q = kvo % 2
                    for c in range(NCH):
                        t_pe = P.op("pe", lambda e, c=c, i=i, q=q, ts=ts: e.matmul(
                            pv[q][:, 0:VW], lhsT=hT[:, c, ts * 128:(ts + 1) * 128], rhs=wvs[i][:, c, :],
                            start=(c == 0), stop=(c == NCH - 1)),
                            waits=[t_wv, t_h, tok_pvfree[q]] if c == 0 else [],
                            inc=(s_pev, 1) if c == NCH - 1 else None)
                    last_pe = t_pe
                    t_vo = P.op("act", lambda e, q=q: e.activation(out=vo[q][:], in_=pv[q][:, 0:VW], func=AF.Copy),
                                waits=[t_pe, tok_vofree[q]], inc=(s_act, 1))
                    tok_pvfree[q] = t_vo
                    tok_vofree[q] = P.op("sync", lambda e, q=q, g=g, b=b, s0=s0, ts=ts: e.dma_start(
                        out=vr[b * 3 + g][s0 + ts * 128:s0 + (ts + 1) * 128, :], in_=vo[q][:]),
                        waits=[t_vo], inc=(s_stv[q], 16))
                    kvo += 1
                tok_wvfree[i] = t_pe
                kv += 1
            tok_hfree = last_pe
        done = [tok_rofree[0], tok_rofree[1], tok_vofree[0], tok_vofree[1]]
        P.barrier(scr, done)
        P.flush()
        return done


def phase_attn2(P, cfg, qk_d, v_d, wo, mask_sb, ones_b, y_part, scr, ready):
    nc = P.nc
    D, T, NC, S, B = cfg["D"], cfg["T"], cfg["NC"], cfg["S"], cfg["B"]
    NCH = D // 128
    NHL = D // 128 // NC
    NTOK = NC * T
    VW = NHL * 128
    DMAX = 16
    SCALE = 128.0 ** -0.5
    P.uid += 1
    U = f"_{P.uid}"
    with ExitStack() as es:
        wos = es.enter_context(nc.sbuf_tensor("a2_wo" + U, [128, NHL, D], BF16))
        oT = es.enter_context(nc.sbuf_tensor("a2_oT" + U, [128, NHL, S], BF16))
        oacc = es.enter_context(nc.sbuf_tensor("a2_oacc" + U, [128, S], F32))
        dacc = es.enter_context(nc.sbuf_tensor("a2_dacc" + U, [128, S], F32))
        Qt = [es.enter_context(nc.sbuf_tensor(f"a2_Qt{i}" + U, [128, S], BF16)) for i in range(2)]
        Kt = [es.enter_context(nc.sbuf_tensor(f"a2_Kt{i}" + U, [128, S + 128 * DMAX], BF16)) for i in range(2)]
        NBMAX = max((S // d // 128 + 1) * d for _, d in GROUPS)
        Vb = [es.enter_context(nc.sbuf_tensor(f"a2_Vb{i}" + U, [128, NBMAX, 128], BF16)) for i in range(2)]
        pe_ = [es.enter_context(nc.sbuf_tensor(f"a2_pe{i}" + U, [128, 512], BF16)) for i in range(4)]
        pm = [es.enter_context(nc.sbuf_tensor(f"a2_pm{i}" + U, [128, 512], BF16)) for i in range(4)]
        ps = [es.enter_context(nc.psum_tensor(f"a2_ps{i}" + U, [128, 512], F32)) for i in range(4)]
        po = es.enter_context(nc.psum_tensor("a2_po" + U, [128, 512], F32))
        pd = es.enter_context(nc.psum_tensor("a2_pd" + U, [128, 512], F32))
        dn = Down(P, es, U, NCH, "a2")
        s_w = P.S("a2_w")
        s_ld = [P.S(f"a2_ld{i}") for i in range(2)]
        s_ms = P.S("a2_ms")
        s_pes = P.S("a2_pes")
        s_pev = P.S("a2_pev")
        s_act = P.S("a2_act")
        s_dve = P.S("a2_dve")
        s_pool = P.S("a2_pool")
        t_w = P.op("sync", lambda e: e.dma_start(out=wos[:], in_=wo.ap().rearrange("(j p) d -> p j d", p=128)),
                   waits=ready, inc=(s_w, 16))
        yr = y_part.ap().rearrange("(r c p) t -> r c p t", p=128, c=NCH)
        qkr = qk_d.ap().rearrange("(x p) s -> x p s", p=128)
        vr = v_d.ap().rearrange("(x s) f -> x s f", s=S)
        tok_ldfree = [None, None]
        tok_psfree = [None] * 4
        tok_pefree = [None] * 4
        tok_pmfree = [None] * 4
        tok_pofree = None
        tok_accfree = None
        tok_oTfree = None
        li = 0
        si = 0
        for b in range(B):
            t_norm = None
            for hl in range(NHL):
                t_z1 = P.op("pool", lambda e: e.memset(oacc[:], 0.0), waits=[tok_accfree], inc=(s_pool, 1))
                t_z2 = P.op("pool", lambda e: e.memset(dacc[:], 0.0), inc=(s_pool, 1))
                t_accw = t_z2
                for g, (win, d) in enumerate(GROUPS):
                    L = S // d
                    nqb = L // 128
                    PADK = 64 * d
                    i = li % 2
                    li += 1
                    xq = ((b * 3 + g) * 2 + 0) * NHL + hl
                    xk = ((b * 3 + g) * 2 + 1) * NHL + hl
                    lw = list(ready) + [tok_ldfree[i]]
                    P.op("pool", lambda e, i=i, PADK=PADK: e.memset(Kt[i][:, 0:PADK], 0.0), waits=lw, inc=(s_pool, 1))
                    P.op("pool", lambda e, i=i, PADK=PADK: e.memset(Kt[i][:, PADK + S:PADK + S + PADK], 0.0),
                         inc=(s_pool, 1))
                    Vv = Vb[i][:, 0:(nqb + 1) * d, :].rearrange("p (c m) e -> p c m e", c=d)
                    P.op("pool", lambda e, Vv=Vv: e.memset(Vv[0:64, :, 0, :], 0.0), inc=(s_pool, 1))
                    t_ms = P.op("pool", lambda e, Vv=Vv, nqb=nqb: e.memset(Vv[64:128, :, nqb, :], 0.0), inc=(s_pool, 1))
                    P.op("sync", lambda e, i=i, xq=xq: e.dma_start(out=Qt[i][:], in_=qkr[xq]), waits=lw, inc=(s_ld[i], 16))
                    P.op("sync", lambda e, i=i, xk=xk, PADK=PADK: e.dma_start(out=Kt[i][:, PADK:PADK + S], in_=qkr[xk]),
                         inc=(s_ld[i], 16))
                    vsrc = vr[b * 3 + g][:, hl * 128:(hl + 1) * 128]
                    if nqb > 1:
                        for c in range(d):
                            P.op("sync", lambda e, Vv=Vv, vsrc=vsrc, d=d, nqb=nqb, c=c: e.dma_start(
                                out=Vv[:, c, 1:nqb, :],
                                in_=vsrc[64 * d:S - 64 * d, :].rearrange("(m k c) e -> k c m e", k=128, c=d)[:, c, :, :]),
                                inc=(s_ld[i], 16))
                    P.op("sync", lambda e, Vv=Vv, vsrc=vsrc, d=d: e.dma_start(
                        out=Vv[64:128, :, 0, :], in_=vsrc[0:64 * d, :].rearrange("(k c) e -> k c e", c=d)),
                        waits=[t_ms], inc=(s_ld[i], 16))
                    t_ld = P.op("sync", lambda e, Vv=Vv, vsrc=vsrc, d=d, nqb=nqb: e.dma_start(
                        out=Vv[0:64, :, nqb, :], in_=vsrc[S - 64 * d:S, :].rearrange("(k c) e -> k c e", c=d)),
                        inc=(s_ld[i], 16))
                    if d == 1:
                        sups = [[(0, jb) for jb in range(j4, j4 + 4)] for j4 in range(0, nqb, 4)]
                    else:
                        sups = [[(c, jb) for c in range(c4, c4 + 4)] for jb in range(nqb) for c4 in range(0, d, 4)]
                    last_pe = None
                    for units in sups:
                        a0 = (si * 2) % 4
                        a1 = a0 + 1
                        si += 1
                        for u, (c, jb) in enumerate(units):
                            j0 = jb * 128
                            qs = j0 * d + c
                            k0s = j0 * d + c
                            k1s = 128 * d + j0 * d + c
                            P.op("pe", lambda e, i=i, u=u, qs=qs, k0s=k0s, d=d, a0=a0: e.matmul(
                                ps[a0][:, u * 128:(u + 1) * 128],
                                lhsT=Kt[i][:, k0s:k0s + 127 * d + 1:d], rhs=Qt[i][:, qs:qs + 127 * d + 1:d],
                                start=True, stop=True),
                                waits=[t_ld, tok_psfree[a0]] if u == 0 else [])
                        for u, (c, jb) in enumerate(units):
                            j0 = jb * 128
                            qs = j0 * d + c
                            k1s = 128 * d + j0 * d + c
                            t_s = P.op("pe", lambda e, i=i, u=u, qs=qs, k1s=k1s, d=d, a1=a1: e.matmul(
                                ps[a1][:, u * 128:(u + 1) * 128],
                                lhsT=Kt[i][:, k1s:k1s + 127 * d + 1:d], rhs=Qt[i][:, qs:qs + 127 * d + 1:d],
                                start=True, stop=True),
                                waits=[tok_psfree[a1]] if u == 0 else [],
                                inc=(s_pes, 1) if u == 3 else None)
                        jb_first = units[0][1] == 0
                        jb_last = units[-1][1] == nqb - 1
                        if d == 1:
                            m0 = 1 if jb_first else 0
                            m1 = 4 if jb_last else 3
                        else:
                            m0 = 2 if jb_first else 0
                            m1 = 5 if jb_last else 3
                        t_e0 = P.op("act", lambda e, a0=a0: e.activation(out=pe_[a0][:], in_=ps[a0][:], func=AF.Exp,
                                                                        scale=SCALE),
                                    waits=[t_s, tok_pefree[a0]], inc=(s_act, 1))
                        t_e1 = P.op("act", lambda e, a1=a1: e.activation(out=pe_[a1][:], in_=ps[a1][:], func=AF.Exp,
                                                                        scale=SCALE),
                                    waits=[tok_pefree[a1]], inc=(s_act, 1))
                        tok_psfree[a0] = t_e0
                        tok_psfree[a1] = t_e1
                        t_m0 = P.op("dve", lambda e, a0=a0, m0=m0: e.tensor_tensor(
                            out=pm[a0][:], in0=pe_[a0][:], in1=mask_sb[:, m0, :], op=ALU.mult),
                            waits=[t_e0, tok_pmfree[a0]], inc=(s_dve, 1))
                        t_m1 = P.op("dve", lambda e, a1=a1, m1=m1: e.tensor_tensor(
                            out=pm[a1][:], in0=pe_[a1][:], in1=mask_sb[:, m1, :], op=ALU.mult),
                            waits=[t_e1, tok_pmfree[a1]], inc=(s_dve, 1))
                        tok_pefree[a0] = t_m0
                        tok_pefree[a1] = t_m1
                        for u, (c, jb) in enumerate(units):
                            P.op("pe", lambda e, u=u, c=c, jb=jb, Vv=Vv, a0=a0: e.matmul(
                                po[:, u * 128:(u + 1) * 128], lhsT=Vv[:, c, jb, :], rhs=pm[a0][:, u * 128:(u + 1) * 128],
                                start=True, stop=False),
                                waits=[t_m0, t_m1, tok_pofree] if u == 0 else [])
                            P.op("pe", lambda e, u=u, c=c, jb=jb, Vv=Vv, a1=a1: e.matmul(
                                po[:, u * 128:(u + 1) * 128], lhsT=Vv[:, c, jb + 1, :],
                                rhs=pm[a1][:, u * 128:(u + 1) * 128], start=False, stop=True))
                        P.op("pe", lambda e, a0=a0: e.matmul(pd[:], lhsT=ones_b[:], rhs=pm[a0][:], start=True, stop=False))
                        t_pv = P.op("pe", lambda e, a1=a1: e.matmul(pd[:], lhsT=ones_b[:], rhs=pm[a1][:],
                                                                  start=False, stop=True), inc=(s_pev, 1))
                        tok_pmfree[a0] = t_pv
                        tok_pmfree[a1] = t_pv
                        last_pe = t_pv
                        c0_, jb0 = units[0]
                        if d == 1:
                            st = jb0 * 128
                            ov = oacc[:, st:st + 512]
                            dv = dacc[:, st:st + 512]
                            pov, pdv = po[:], pd[:]
                        else:
                            st = jb0 * 128 * d
                            ov = oacc[:, st:st + 128 * d].rearrange("p (q c) -> p c q", c=d)[:, c0_:c0_ + 4, :]
                            dv = dacc[:, st:st + 128 * d].rearrange("p (q c) -> p c q", c=d)[:, c0_:c0_ + 4, :]
                            pov = po[:].rearrange("p (c q) -> p c q", c=4)
                            pdv = pd[:].rearrange("p (c q) -> p c q", c=4)
                        t_a1 = P.op("dve", lambda e, ov=ov, pov=pov: e.tensor_tensor(out=ov, in0=ov, in1=pov, op=ALU.add),
                                    waits=[t_pv, t_accw], inc=(s_dve, 1))
                        t_a2 = P.op("dve", lambda e, dv=dv, pdv=pdv: e.tensor_tensor(out=dv, in0=dv, in1=pdv, op=ALU.add),
                                    waits=[t_accw], inc=(s_dve, 1))
                        tok_pofree = t_a2
                        t_accw = t_a2
                    tok_ldfree[i] = last_pe
                t_n1 = P.op("dve", lambda e: e.reciprocal(out=dacc[:], in_=dacc[:]), waits=[t_accw], inc=(s_dve, 1))
                t_norm = P.op("dve", lambda e, hl=hl: e.tensor_tensor(out=oT[:, hl, :], in0=oacc[:], in1=dacc[:],
                                                                    op=ALU.mult),
                              waits=[t_n1, tok_oTfree] if hl == 0 else [t_n1], inc=(s_dve, 1))
                tok_accfree = t_norm
            for s0 in range(0, S, 512):
                n0 = b * S + s0
                r, c0 = tile_of(cfg, n0)
                tok_oTfree = dn.run(wos, oT, NHL, lambda ic, r=r, c0=c0: yr[r][ic][:, c0:c0 + 512],
                                    [t_norm, t_w], zsl=slice(s0, s0 + 512))
        done = dn.done()
        P.barrier(scr, done)
        P.flush()
        return done


def make_cfg(D, F, B, S, DEPTH, NC=NCORES):
    T = B * S // NC
    fs = F // NC
    NJ = -(-fs // 128)
    return dict(D=D, F=F, B=B, S=S, DEPTH=DEPTH, NC=NC, T=T, NJ=NJ, FS=fs, NG=3 * DEPTH + 1)


def build_program(cfg):
    nc = bass.Bass("TRN2", target_bir_lowering=False)
    D, T, NJ, NC, S, B, DEPTH = cfg["D"], cfg["T"], cfg["NJ"], cfg["NC"], cfg["S"], cfg["B"], cfg["DEPTH"]
    NCH = D // 128
    NK = D // NC // 128
    NHL = NK
    NQK = 6 * NHL
    VW = NHL * 128
    NTOK = NC * T
    NFF = 2 * DEPTH
    n_conv = len(range(0, DEPTH, 2))
    n_attn = len(range(1, DEPTH, 2))

    def din(name, shape, dt=F32):
        return nc.dram_tensor(name, shape, dt, kind="ExternalInput")

    x_in = din("x_in", [D, T])
    out = nc.dram_tensor("out", [D, T], F32, kind="ExternalOutput")
    gains = din("gains", [128, cfg["NG"] * NCH])
    wg_in = [din(f"wg{i}", [NJ * 128, D]) for i in range(NFF)]
    wu_in = [din(f"wu{i}", [NJ * 128, D]) for i in range(NFF)]
    wd_in = [din(f"wd{i}", [NJ * 128, D]) for i in range(NFF)]
    wg_b = [nc.dram_tensor(f"wgb{i}", [NJ * 128, D], BF16) for i in range(NFF)]
    wu_b = [nc.dram_tensor(f"wub{i}", [NJ * 128, D], BF16) for i in range(NFF)]
    wd_b = [nc.dram_tensor(f"wdb{i}", [NJ * 128, D], BF16) for i in range(NFF)]
    pw1_in = [din(f"pw1_{i}", [2 * NK * 128, D]) for i in range(n_conv)]
    pw2_in = [din(f"pw2_{i}", [NK * 128, D]) for i in range(n_conv)]
    cvp_in = [din(f"cvp_{i}", [128, NK * 34]) for i in range(n_conv)]
    lnp_in = [din(f"lnp_{i}", [128, NK * 2]) for i in range(n_conv)]
    pw2b_in = [din(f"pw2b_{i}", [128, NCH]) for i in range(n_conv)]
    pw1_b = [nc.dram_tensor(f"pw1b_{i}", [2 * NK * 128, D], BF16) for i in range(n_conv)]
    pw2_b = [nc.dram_tensor(f"pw2bb_{i}", [NK * 128, D], BF16) for i in range(n_conv)]
    wqk_in = [din(f"wqk_{i}", [NQK * 128, D]) for i in range(n_attn)]
    wv_in = [din(f"wv_{i}", [D, 3 * VW]) for i in range(n_attn)]
    wo_in = [din(f"wo_{i}", [VW, D]) for i in range(n_attn)]
    wqk_b = [nc.dram_tensor(f"wqkb_{i}", [NQK * 128, D], BF16) for i in range(n_attn)]
    wv_b = [nc.dram_tensor(f"wvb_{i}", [D, 3 * VW], BF16) for i in range(n_attn)]
    wo_b = [nc.dram_tensor(f"wob_{i}", [VW, D], BF16) for i in range(n_attn)]
    rope_in = din("rope", [128, 2 * S])
    perm_in = din("perm", [128, 128])
    mask_in = din("masks", [128, 6 * 512])

    x_res = nc.dram_tensor("x_res", [D, T], F32)
    h_own = nc.dram_tensor("h_own", [D, T], BF16)
    h_all = nc.dram_tensor("h_all", [NC * D, T], BF16)
    y_part = nc.dram_tensor("y_part", [NC * D, T], F32)
    y_rs = nc.dram_tensor("y_rs", [D, T], F32)
    conv_out = nc.dram_tensor("conv_out", [NK * 128, NTOK], F32)
    SPAD = max(2 * NTOK // 128, 2048)
    stats_own = nc.dram_tensor("stats_own", [128, SPAD], F32)
    stats_all = nc.dram_tensor("stats_all", [NC * 128, SPAD], F32)
    qk_d = nc.dram_tensor("qk_d", [B * 3 * 2 * NHL * 128, S], BF16)
    v_d = nc.dram_tensor("v_d", [B * 3 * S, VW], BF16)

    with ExitStack() as es:
        P = Prog(nc, es)
        scr = es.enter_context(nc.sbuf_tensor("scr", [128, 4], F32))
        ones_f = es.enter_context(nc.sbuf_tensor("ones_f", [128, 128], F32))
        ones_b = es.enter_context(nc.sbuf_tensor("ones_b", [128, 128], BF16))
        eps_sb = es.enter_context(nc.sbuf_tensor("eps_sb", [128, 2], F32))
        g_sb = es.enter_context(nc.sbuf_tensor("g_sb", [128, cfg["NG"] * NCH], F32))
        pw2b_sb = es.enter_context(nc.sbuf_tensor("pw2b_sb", [128, max(1, n_conv) * NCH], F32))
        s_init = P.S("init")
        P.op("dve", lambda e: e.memset(ones_f[:], 1.0), inc=(s_init, 1))
        P.op("dve", lambda e: e.memset(ones_b[:], 1.0), inc=(s_init, 1))
        P.op("dve", lambda e: e.memset(scr[:], 0.0), inc=(s_init, 1))
        P.op("dve", lambda e: e.memset(eps_sb[:, 0:1], RMS_EPS), inc=(s_init, 1))
        P.op("dve", lambda e: e.memset(eps_sb[:, 1:2], LN_EPS), inc=(s_init, 1))
        t_init = (s_init, s_init.val)
        s_g = P.S("ldg")
        P.op("sync", lambda e: e.dma_start(out=g_sb[:], in_=gains[:, :]), inc=(s_g, 16))
        for i in range(n_conv):
            P.op("sync", lambda e, i=i: e.dma_start(out=pw2b_sb[:, i * NCH:(i + 1) * NCH], in_=pw2b_in[i][:, :]),
                 inc=(s_g, 16))
        t_g = (s_g, s_g.val)
        s_xc = P.S("xcopy")
        t_xc = P.op("sync", lambda e: e.dma_start(out=x_res[:, :], in_=x_in[:, :]), inc=(s_xc, 16))
        with ExitStack() as es3:
            zt = es3.enter_context(nc.sbuf_tensor("zt_init", [128, SPAD], F32))
            s_z = P.S("zinit")
            t_z = P.op("dve", lambda e: e.memset(zt[:], 0.0), inc=(s_z, 1))
            s_zd = P.S("zinit_d")
            t_zd = P.op("sync", lambda e: e.dma_start(out=stats_own[:, :], in_=zt[:]), waits=[t_z], inc=(s_zd, 16))
            for eng in ("dve", "act", "pe", "sync", "pool"):
                P.wait(eng, [t_init, t_g, t_xc, t_zd])
            P.flush()
        s_cast = P.S("cast")
        t_ff = [None] * NFF
        t_cv = [None] * n_conv
        t_at = [None] * n_attn

        def cast_ff(i):
            s_cast = P.S(f"cast_ff{i}")
            cast_dram(P, wg_in[i], wg_b[i], NJ * 128, D, s_cast)
            cast_dram(P, wu_in[i], wu_b[i], NJ * 128, D, s_cast)
            t_ff[i] = cast_dram(P, wd_in[i], wd_b[i], NJ * 128, D, s_cast)

        for li in range(DEPTH):
            cast_ff(2 * li)
            j = li // 2
            s_cast = P.S(f"cast_mx{li}")
            if li % 2 == 0:
                cast_dram(P, pw1_in[j], pw1_b[j], 2 * NK * 128, D, s_cast)
                t_cv[j] = cast_dram(P, pw2_in[j], pw2_b[j], NK * 128, D, s_cast)
            else:
                cast_dram(P, wqk_in[j], wqk_b[j], NQK * 128, D, s_cast)
                cast_dram(P, wv_in[j], wv_b[j], D, 3 * VW, s_cast)
                t_at[j] = cast_dram(P, wo_in[j], wo_b[j], VW, D, s_cast)
            cast_ff(2 * li + 1)

        def gsl(i):
            return g_sb[:, i * NCH:(i + 1) * NCH]

        state = dict(y=None, scale=0.0, bias=None, gi=0)

        def norm_and_gather():
            done = phase_rn(P, es, cfg, x_res, state["y"], state["scale"], state["bias"], gsl(state["gi"]),
                            h_own, None, scr, ones_f, eps_sb=eps_sb)
            state["gi"] += 1
            return coll(P, "AllGather", h_own[:, :], h_all[:, :], done, NC)

        def reduce(done, scale, bias=None):
            t_rs = coll(P, "ReduceScatter", y_part[:, :], y_rs[:, :], done, NC)
            for eng in ("sync", "dve", "act", "pool", "pe"):
                P.wait(eng, [t_rs])
            state.update(y=y_rs, scale=scale, bias=bias)

        for li in range(DEPTH):
            j = li // 2
            t_ag = norm_and_gather()
            done = phase_ffn(P, cfg, h_all, wg_b[2 * li], wu_b[2 * li], wd_b[2 * li], y_part, scr, [t_ag, t_ff[2 * li]])
            reduce(done, 0.5)
            SKIP = cfg.get("SKIP", "")
            if ("conv" in SKIP and li % 2 == 0) or ("attn" in SKIP and li % 2 == 1):
                state["gi"] += 1
                t_ag = norm_and_gather()
                done = phase_ffn(P, cfg, h_all, wg_b[2 * li + 1], wu_b[2 * li + 1], wd_b[2 * li + 1], y_part, scr,
                                 [t_ag, t_ff[2 * li + 1]])
                reduce(done, 0.5)
                continue
            t_ag = norm_and_gather()
            if li % 2 == 0:
                with ExitStack() as es2:
                    cvp = es2.enter_context(nc.sbuf_tensor(f"cvp_sb{li}", [128, NK, 34], F32))
                    lnp = es2.enter_context(nc.sbuf_tensor(f"lnp_sb{li}", [128, NK, 2], F32))
                    s_cp = P.S("ldcp")
                    P.op("sync", lambda e: e.dma_start(out=cvp[:], in_=cvp_in[j].ap().rearrange("p (k w) -> p k w", w=34)),
                         inc=(s_cp, 16))
                    t_cp = P.op("sync", lambda e: e.dma_start(
                        out=lnp[:], in_=lnp_in[j].ap().rearrange("p (k w) -> p k w", w=2)), inc=(s_cp, 16))
                    for eng in ("dve", "act", "pool"):
                        P.wait(eng, [t_cp])
                    done = phase_conv1(P, cfg, h_all, pw1_b[j], cvp, conv_out, stats_own, scr, ones_f,
                                       [t_ag, t_cv[j]])
                    DBG = cfg.get("DBG", "")
                    if DBG == "conv1":
                        for eng in ("sync", "pool"):
                            P.wait(eng, done)
                        state["gi"] += 0
                        t_ag = norm_and_gather()
                        done = phase_ffn(P, cfg, h_all, wg_b[2 * li + 1], wu_b[2 * li + 1], wd_b[2 * li + 1], y_part, scr,
                                         [t_ag, t_ff[2 * li + 1]])
                        reduce(done, 0.5)
                        continue
                    t_sg = coll(P, "AllGather", stats_own[:, :], stats_all[:, :], done, NC)
                    if DBG == "conv1ag":
                        for eng in ("sync", "pool"):
                            P.wait(eng, [t_sg])
                        t_ag = norm_and_gather()
                        done = phase_ffn(P, cfg, h_all, wg_b[2 * li + 1], wu_b[2 * li + 1], wd_b[2 * li + 1], y_part, scr,
                                         [t_ag, t_ff[2 * li + 1]])
                        reduce(done, 0.5)
                        continue
                    done = phase_conv2(P, cfg, conv_out, stats_all, lnp, pw2_b[j], y_part, scr, ones_f, eps_sb,
                                       [t_sg])
                reduce(done, 1.0, pw2b_sb[:, j * NCH:(j + 1) * NCH])
            else:
                with ExitStack() as es2:
                    rope_sb = es2.enter_context(nc.sbuf_tensor(f"rope_sb{li}", [128, 2, S], F32))
                    perm_sb = es2.enter_context(nc.sbuf_tensor(f"perm_sb{li}", [128, 128], BF16))
                    s_cp = P.S("ldcp")
                    P.op("sync", lambda e: e.dma_start(out=rope_sb[:], in_=rope_in.ap().rearrange("p (k s) -> p k s", k=2)),
                         inc=(s_cp, 16))
                    t_cp0 = (s_cp, s_cp.val)
                    s_cp2 = P.S("ldcp_p")
                    t_cp = [t_cp0, P.op("pool", lambda e: e.dma_start(out=perm_sb[:], in_=perm_in[:, :]), inc=(s_cp2, 16))]
                    for eng in ("dve", "act", "pool", "pe"):
                        P.wait(eng, t_cp)
                    done = phase_attn1(P, cfg, h_all, wqk_b[j], wv_b[j], rope_sb, perm_sb, qk_d, v_d, scr,
                                       [t_ag, t_at[j]])
                if cfg.get("DBG", "") == "attn1":
                    for eng in ("sync", "pool"):
                        P.wait(eng, done)
                    t_ag = norm_and_gather()
                    done = phase_ffn(P, cfg, h_all, wg_b[2 * li + 1], wu_b[2 * li + 1], wd_b[2 * li + 1], y_part, scr,
                                     [t_ag, t_ff[2 * li + 1]])
                    reduce(done, 0.5)
                    continue
                with ExitStack() as es2:
                    mask_sb = es2.enter_context(nc.sbuf_tensor(f"mask_sb{li}", [128, 6, 512], BF16))
                    s_cp = P.S("ldcp_p")
                    t_cp = P.op("pool", lambda e: e.dma_start(
                        out=mask_sb[:], in_=mask_in.ap().rearrange("p (k s) -> p k s", k=6)), inc=(s_cp, 16))
                    for eng in ("dve", "act", "pool", "pe", "sync"):
                        P.wait(eng, [t_cp] + list(done))
                    done = phase_attn2(P, cfg, qk_d, v_d, wo_b[j], mask_sb, ones_b, y_part, scr, [t_cp])
                reduce(done, 1.0, None)
            t_ag = norm_and_gather()
            done = phase_ffn(P, cfg, h_all, wg_b[2 * li + 1], wu_b[2 * li + 1], wd_b[2 * li + 1], y_part, scr,
                             [t_ag, t_ff[2 * li + 1]])
            reduce(done, 0.5)
        done = phase_rn(P, es, cfg, x_res, state["y"], state["scale"], state["bias"], gsl(state["gi"]),
                        None, out, scr, ones_f, eps_sb=eps_sb)
        P.wait("sync", done)
        P.wait("pool", done)
        P.flush()
    return nc


def _tile_major(w, nj):
    D = w.shape[0]
    nch = D // 128
    t = w.reshape(nch, 128, nj, 128).transpose(2, 1, 0, 3)
    return np.ascontiguousarray(t.reshape(nj * 128, nch * 128))


def _pp(v, nch):
    return np.ascontiguousarray(v.reshape(nch, 128).T)


def prep_inputs(inp, cfg):
    D, T, NJ, NC, S, B, DEPTH, F = (cfg[k] for k in ("D", "T", "NJ", "NC", "S", "B", "DEPTH", "F"))
    NCH = D // 128
    NK = D // NC // 128
    NHL = NK
    CC = NK * 128
    H = D // 128
    fs = cfg["FS"]
    f32 = np.float32
    x = np.asarray(inp["x"], f32).reshape(B * S, D)
    gl = []
    for li in range(DEPTH):
        gl += [inp["ffn1_norm"][li], inp["mix_norm"][li], inp["ffn2_norm"][li]]
    gl.append(inp["final_norm"])
    gains = np.concatenate([_pp(np.asarray(g, f32), NCH) for g in gl], axis=1)
    inv = (10000.0 ** (-np.arange(0, 128, 2, dtype=f32) / f32(128))).astype(f32)
    ang = (np.arange(S, dtype=f32)[:, None] * inv[None, :]).astype(f32)
    cos, sin = np.cos(ang).astype(f32).T, np.sin(ang).astype(f32).T
    rope = np.concatenate([np.concatenate([cos, cos], 0), np.concatenate([-sin, sin], 0)], axis=1)
    perm = np.zeros((128, 128), f32)
    for m in range(128):
        perm[(m + 64) % 128, m] = 1.0
    kk = np.arange(128)[:, None]
    qq = np.arange(128)[None, :]
    m0 = (kk >= qq).astype(f32)
    m0f = m0 * (kk >= 64)
    m1 = (kk <= qq).astype(f32)
    m1l = m1 * (kk < 64)
    masks = np.concatenate([
        np.tile(m0, (1, 4)), np.concatenate([m0f, m0, m0, m0], 1), np.tile(m0f, (1, 4)),
        np.tile(m1, (1, 4)), np.concatenate([m1, m1, m1, m1l], 1), np.tile(m1l, (1, 4))], axis=1).astype(f32)
    rope = np.ascontiguousarray(rope, f32)
    maps = []
    for r in range(NC):
        m = {"x_in": np.ascontiguousarray(x[r * T:(r + 1) * T].T), "gains": gains,
             "rope": rope, "perm": perm, "masks": masks}
        for li in range(DEPTH):
            for k, nm in enumerate(("ffn1", "ffn2")):
                i = 2 * li + k
                for wn, key in (("wg", f"{nm}_wg"), ("wu", f"{nm}_wu")):
                    w = np.zeros((D, NJ * 128), f32)
                    w[:, :fs] = inp[key][li][:, r * fs:(r + 1) * fs]
                    m[f"{wn}{i}"] = _tile_major(w, NJ)
                w = np.zeros((NJ * 128, D), f32)
                w[:fs] = inp[f"{nm}_wd"][li][r * fs:(r + 1) * fs]
                m[f"wd{i}"] = w
            j = li // 2
            if li % 2 == 0:
                pw1 = inp["conv_pw1"][j]
                w = np.concatenate([pw1[:, r * CC:(r + 1) * CC], pw1[:, D + r * CC:D + (r + 1) * CC]], axis=1)
                m[f"pw1_{j}"] = _tile_major(np.ascontiguousarray(w, f32), 2 * NK)
                m[f"pw2_{j}"] = np.ascontiguousarray(inp["conv_pw2"][j][r * CC:(r + 1) * CC], f32)
                cvp = np.zeros((128, NK, 34), f32)
                b1 = np.asarray(inp["conv_pw1_b"][j], f32)
                cvp[:, :, 0] = _pp(b1[r * CC:(r + 1) * CC], NK)
                cvp[:, :, 1] = _pp(b1[D + r * CC:D + (r + 1) * CC], NK)
                cvp[:, :, 2] = _pp(np.asarray(inp["conv_dw_b"][j], f32)[r * CC:(r + 1) * CC], NK)
                dw = np.asarray(inp["conv_dw"][j], f32)
                for w_ in range(31):
                    cvp[:, :, 3 + w_] = _pp(dw[w_, r * CC:(r + 1) * CC], NK)
                m[f"cvp_{j}"] = cvp.reshape(128, NK * 34)
                lnp = np.zeros((128, NK, 2), f32)
                lnp[:, :, 0] = _pp(np.asarray(inp["conv_ln_g"][j], f32)[r * CC:(r + 1) * CC], NK)
                lnp[:, :, 1] = _pp(np.asarray(inp["conv_ln_b"][j], f32)[r * CC:(r + 1) * CC], NK)
                m[f"lnp_{j}"] = lnp.reshape(128, NK * 2)
                m[f"pw2b_{j}"] = _pp(np.asarray(inp["conv_pw2_b"][j], f32), NCH)
            else:
                wqkv = inp["attn_wqkv"][j]
                cols = []
                for g in range(3):
                    for w_ in range(2):
                        for hl in range(NHL):
                            base = ((g * 3 + w_) * H + r * NHL + hl) * 128
                            cols.append(wqkv[:, base:base + 128])
                m[f"wqk_{j}"] = _tile_major(np.ascontiguousarray(np.concatenate(cols, 1), f32), 6 * NHL)
                cols = []
                for g in range(3):
                    base = ((g * 3 + 2) * H + r * NHL) * 128
                    cols.append(wqkv[:, base:base + NHL * 128])
                m[f"wv_{j}"] = np.ascontiguousarray(np.concatenate(cols, 1), f32)
                m[f"wo_{j}"] = np.ascontiguousarray(inp["attn_wo"][j][r * NHL * 128:(r + 1) * NHL * 128], f32)
        maps.append(m)
    return maps


_NC_CACHE = {}


def run(inp, cfg):
    key = tuple(sorted((k, str(v)) for k, v in cfg.items()))
    if key not in _NC_CACHE:
        _NC_CACHE[key] = build_program(cfg)
    nc = _NC_CACHE[key]
    maps = prep_inputs(inp, cfg)
    res = run_bass_kernel_spmd(nc, maps, core_ids=list(range(cfg["NC"])))
    T = cfg["T"]
    out = np.concatenate([np.asarray(res.results[r]["out"]).T for r in range(cfg["NC"])], axis=0)
    return np.ascontiguousarray(out.reshape(cfg["B"], cfg["S"], cfg["D"]).astype(np.float32))


def kernel(**inputs):
    cfg = make_cfg(D=4096, F=11008, B=2, S=4096, DEPTH=2)
    inp = {k: np.asarray(v) for k, v in inputs.items()}
    return run(inp, cfg)
```
